# Optimizing a Trainium2 kernel written in Bass

```python
import jax, jax.numpy as jnp
from jax import lax
import numpy as np

D_MODEL = 1024
BATCH = 2
SEQ = 8192
DEPTH = 2

ROPE_THETA = 10000.0
NORM_EPS = 1e-6
GN_EPS = 1e-5

RET_HEADS = 4
RET_DK = 64
RET_DV = 128
RET_CHUNK = 128

NSA_HEADS = 8
NSA_KV_HEADS = 2
NSA_DK = 64
NSA_GROUP = NSA_HEADS // NSA_KV_HEADS
CMP_BLOCK = 32
CMP_STRIDE = 16
CMP_HID = 256
SLC_BLOCK = 64
SLC_TOPK = 16
SLC_LOCAL = 2
WINDOW = 512
NSA_QBLOCK = 128
SEL_BIG = 1e9
NEG = -1e30

D_RET = RET_HEADS * RET_DV
D_NSA = NSA_HEADS * NSA_DK
D_MIX = D_RET + D_NSA

RET_QK = RET_HEADS * RET_DK
NSA_KV = NSA_KV_HEADS * NSA_DK
NSA_GATES = 3 * NSA_HEADS
SPLITS = [RET_QK, RET_QK, D_RET, D_RET, D_NSA] + [NSA_KV] * 6 + [NSA_GATES]
IN_COLS = sum(SPLITS)

D_FF = 2816
CONV_WIDTH = 3

kernel_name = 'hybrid_retention_nsa_convffn_sandwich'


def rms_norm(x, w):
    xf = x.astype(jnp.float32)
    y = xf * lax.rsqrt(jnp.mean(xf * xf, axis=-1, keepdims=True) + NORM_EPS)
    return (y * w.astype(jnp.float32)).astype(x.dtype)


def rope_angles(pos, dim):
    inv = 1.0 / (ROPE_THETA ** (jnp.arange(0, dim, 2, dtype=jnp.float32) / dim))
    ang = pos.astype(jnp.float32)[:, None] * inv[None, :]
    return jnp.cos(ang), jnp.sin(ang)


def apply_rope(x, cos, sin):
    x1, x2 = jnp.split(x, 2, axis=-1)
    c = cos[:, None, :]
    s = sin[:, None, :]
    return jnp.concatenate([x1 * c - x2 * s, x1 * s + x2 * c], axis=-1)


def masked_softmax(s, mask):
    s = jnp.where(mask, s, NEG)
    m = jnp.max(s, axis=-1, keepdims=True)
    e = jnp.where(mask, jnp.exp(s - m), 0.0)
    return e / jnp.maximum(jnp.sum(e, axis=-1, keepdims=True), 1e-30)


def retention(q, k, v, g, gn_w, cos, sin):
    f32 = jnp.float32
    B, S = q.shape[0], q.shape[1]
    H, C = RET_HEADS, RET_CHUNK
    N = S // C
    q = apply_rope(q.astype(f32), cos, sin)
    k = apply_rope(k.astype(f32), cos, sin) * (RET_DK ** -0.5)
    v = v.astype(f32).reshape(B, S, H, RET_DV)
    qc = q.reshape(B, N, C, H, RET_DK).transpose(0, 3, 1, 2, 4)
    kc = k.reshape(B, N, C, H, RET_DK).transpose(0, 3, 1, 2, 4)
    vc = v.reshape(B, N, C, H, RET_DV).transpose(0, 3, 1, 2, 4)
    log_gamma = jnp.log(1.0 - 2.0 ** (-5.0 - jnp.arange(H, dtype=f32)))
    i = jnp.arange(C, dtype=f32)
    diff = i[:, None] - i[None, :]
    decay = jnp.where(diff >= 0, jnp.exp(log_gamma[:, None, None] * jnp.maximum(diff, 0.0)), 0.0)
    scores = jnp.einsum('bhnid,bhnjd->bhnij', qc, kc) * decay[None, :, None]
    o_inner = jnp.einsum('bhnij,bhnje->bhnie', scores, vc)
    zeta = jnp.exp(log_gamma[:, None] * (C - 1.0 - i)[None, :])
    xi = jnp.exp(log_gamma[:, None] * (i + 1.0)[None, :])
    kv = jnp.einsum('bhnjd,bhnje->bhnde', kc * zeta[None, :, None, :, None], vc)
    chunk_decay = jnp.exp(log_gamma * C)[None, :, None, None]

    def step(state, kv_n):
        return state * chunk_decay + kv_n, state

    _, prev = lax.scan(step, jnp.zeros((B, H, RET_DK, RET_DV), f32), kv.transpose(2, 0, 1, 3, 4))
    prev = prev.transpose(1, 2, 0, 3, 4)
    o_cross = jnp.einsum('bhnid,bhnde->bhnie', qc, prev) * xi[None, :, None, :, None]
    o = (o_inner + o_cross).transpose(0, 2, 3, 1, 4).reshape(B, S, H, RET_DV)
    mu = jnp.mean(o, axis=-1, keepdims=True)
    var = jnp.mean(jnp.square(o - mu), axis=-1, keepdims=True)
    o = ((o - mu) * lax.rsqrt(var + GN_EPS)).reshape(B, S, D_RET) * gn_w.astype(f32)
    return jax.nn.silu(g.astype(f32)) * o


def compress(kv, pos_emb, w1, w2):
    B, S, G, D = kv.shape
    n_cmp = (S - CMP_BLOCK) // CMP_STRIDE + 1
    idx = (np.arange(n_cmp)[:, None] * CMP_STRIDE + np.arange(CMP_BLOCK)[None, :]).astype(np.int32)
    blocks = kv[:, idx] + pos_emb.astype(jnp.float32)[None, None, :, None, :]
    flat = blocks.transpose(0, 1, 3, 2, 4).reshape(B, n_cmp, G, CMP_BLOCK * D)
    hid = jax.nn.gelu(flat @ w1.astype(jnp.float32))
    return hid @ w2.astype(jnp.float32)


def nsa(q, k_cmp, v_cmp, k_slc, v_slc, k_win, v_win, gate_logits,
        cmp_k_pos, cmp_k_w1, cmp_k_w2, cmp_v_pos, cmp_v_w1, cmp_v_w2, cos, sin):
    f32 = jnp.float32
    B, S = q.shape[0], q.shape[1]
    G, R, D, QB = NSA_KV_HEADS, NSA_GROUP, NSA_DK, NSA_QBLOCK
    q = apply_rope(q.astype(f32), cos, sin) * (D ** -0.5)
    kc = compress(k_cmp.astype(f32), cmp_k_pos, cmp_k_w1, cmp_k_w2)
    vc = compress(v_cmp.astype(f32), cmp_v_pos, cmp_v_w1, cmp_v_w2)
    n_cmp = kc.shape[1]
    cmp_start = np.arange(n_cmp) * CMP_STRIDE
    cmp_end = (cmp_start + CMP_BLOCK - 1).astype(np.int32)
    ccos, csin = rope_angles(jnp.asarray(cmp_end), D)
    kc = apply_rope(kc, ccos, csin)
    ks = apply_rope(k_slc.astype(f32), cos, sin)
    kw = apply_rope(k_win.astype(f32), cos, sin)
    n_slc = S // SLC_BLOCK
    topk = min(SLC_TOPK, n_slc)
    slc_start = np.arange(n_slc) * SLC_BLOCK
    overlap = ((cmp_start[:, None] < slc_start[None, :] + SLC_BLOCK)
               & (cmp_start[:, None] + CMP_BLOCK > slc_start[None, :])).astype(np.float32)

    qg = q.reshape(B, S, G, R, D).transpose(0, 2, 3, 1, 4)
    kc_g = kc.transpose(0, 2, 1, 3)
    vc_g = vc.transpose(0, 2, 1, 3)
    ks_blk = ks.reshape(B, n_slc, SLC_BLOCK, G, D).transpose(0, 3, 1, 2, 4)
    vs_blk = v_slc.astype(f32).reshape(B, n_slc, SLC_BLOCK, G, D).transpose(0, 3, 1, 2, 4)
    pad = ((0, 0), (0, 0), (WINDOW, 0), (0, 0))
    kw_pad = jnp.pad(kw.transpose(0, 2, 1, 3), pad)
    vw_pad = jnp.pad(v_win.astype(f32).transpose(0, 2, 1, 3), pad)
    nqb = S // QB
    q_blocks = qg.reshape(B, G, R, nqb, QB, D).transpose(3, 0, 1, 2, 4, 5)
    bi = jnp.arange(B)[:, None, None, None]
    gi = jnp.arange(G)[None, :, None, None]
    jsl = jnp.arange(n_slc, dtype=jnp.int32)

    def block_fn(args):
        qb, blk = args
        t = blk * QB + jnp.arange(QB, dtype=jnp.int32)
        valid_c = jnp.asarray(cmp_end)[None, :] <= t[:, None]
        p_cmp = masked_softmax(jnp.einsum('bgrtd,bgnd->bgrtn', qb, kc_g), valid_c)
        o_cmp = jnp.einsum('bgrtn,bgnd->bgrtd', p_cmp, vc_g)
        imp = jnp.einsum('bgrtn,nj->bgtj', p_cmp, overlap)
        qblk = t // SLC_BLOCK
        blk_valid = jsl[None, :] <= qblk[:, None]
        forced = (jsl[None, :] == 0) | (blk_valid & (jsl[None, :] > qblk[:, None] - SLC_LOCAL))
        score = jnp.where(forced, SEL_BIG, jnp.where(blk_valid, imp, -SEL_BIG))
        _, sel = lax.top_k(score, topk)
        kg = ks_blk[bi, gi, sel]
        vg = vs_blk[bi, gi, sel]
        tok = sel[..., None] * SLC_BLOCK + jnp.arange(SLC_BLOCK, dtype=jnp.int32)
        smask = (tok <= t[None, None, :, None, None]).reshape(B, G, 1, QB, topk * SLC_BLOCK)
        s_slc = jnp.einsum('bgrtd,bgtksd->bgrtks', qb, kg).reshape(B, G, R, QB, topk * SLC_BLOCK)
        p_slc = masked_softmax(s_slc, smask).reshape(B, G, R, QB, topk, SLC_BLOCK)
        o_slc = jnp.einsum('bgrtks,bgtksd->bgrtd', p_slc, vg)
        kwb = lax.dynamic_slice_in_dim(kw_pad, blk * QB, WINDOW + QB, axis=2)
        vwb = lax.dynamic_slice_in_dim(vw_pad, blk * QB, WINDOW + QB, axis=2)
        kpos = blk * QB - WINDOW + jnp.arange(WINDOW + QB, dtype=jnp.int32)
        wmask = (kpos[None, :] <= t[:, None]) & (kpos[None, :] > t[:, None] - WINDOW) & (kpos[None, :] >= 0)
        p_win = masked_softmax(jnp.einsum('bgrtd,bgsd->bgrts', qb, kwb), wmask)
        o_win = jnp.einsum('bgrts,bgsd->bgrtd', p_win, vwb)
        return o_cmp, o_slc, o_win

    o_cmp, o_slc, o_win = lax.map(block_fn, (q_blocks, jnp.arange(nqb, dtype=jnp.int32)))

    def to_bshd(o):
        return o.transpose(1, 0, 4, 2, 3, 5).reshape(B, S, NSA_HEADS, D)

    gates = jax.nn.sigmoid(gate_logits.astype(f32)).reshape(B, S, 3, NSA_HEADS)[..., None]
    out = gates[:, :, 0] * to_bshd(o_cmp) + gates[:, :, 1] * to_bshd(o_slc) + gates[:, :, 2] * to_bshd(o_win)
    return out.reshape(B, S, D_NSA)


def conv_ffn(x, w_up, conv_w, w_down):
    u = x @ w_up
    c = u.shape[-1]
    u = lax.conv_general_dilated(u, conv_w[:, None, :].astype(u.dtype), window_strides=(1,),
                                 padding=[(CONV_WIDTH - 1, 0)],
                                 dimension_numbers=('NWC', 'WIO', 'NWC'),
                                 feature_group_count=c)
    gate, val = jnp.split(u, 2, axis=-1)
    return (jax.nn.gelu(gate) * val) @ w_down


def setup_inputs(seed: int = 0) -> dict:
    key = jax.random.key(seed)
    ks = jax.random.split(key, 20)
    L = DEPTH

    def nrm(k, shape, scale):
        return jax.random.normal(k, shape, jnp.float32) * scale

    def gain(k, shape):
        return 1.0 + 0.02 * jax.random.normal(k, shape, jnp.float32)

    return {
        'x': nrm(ks[0], (BATCH, SEQ, D_MODEL), 1.0),
        'norm_mix_pre': gain(ks[1], (L, D_MODEL)),
        'w_in': nrm(ks[2], (L, D_MODEL, IN_COLS), D_MODEL ** -0.5),
        'ret_gn_w': gain(ks[3], (L, D_RET)),
        'cmp_k_pos': nrm(ks[4], (L, CMP_BLOCK, NSA_DK), 0.02),
        'cmp_k_w1': nrm(ks[5], (L, CMP_BLOCK * NSA_DK, CMP_HID), (CMP_BLOCK * NSA_DK) ** -0.5),
        'cmp_k_w2': nrm(ks[6], (L, CMP_HID, NSA_DK), CMP_HID ** -0.5),
        'cmp_v_pos': nrm(ks[7], (L, CMP_BLOCK, NSA_DK), 0.02),
        'cmp_v_w1': nrm(ks[8], (L, CMP_BLOCK * NSA_DK, CMP_HID), (CMP_BLOCK * NSA_DK) ** -0.5),
        'cmp_v_w2': nrm(ks[9], (L, CMP_HID, NSA_DK), CMP_HID ** -0.5),
        'w_out': nrm(ks[10], (L, D_MIX, D_MODEL), D_MIX ** -0.5),
        'norm_mix_post': gain(ks[11], (L, D_MODEL)),
        'norm_ffn_pre': gain(ks[12], (L, D_MODEL)),
        'ffn_w_up': nrm(ks[13], (L, D_MODEL, 2 * D_FF), D_MODEL ** -0.5),
        'ffn_conv': nrm(ks[14], (L, CONV_WIDTH, 2 * D_FF), CONV_WIDTH ** -0.5),
        'ffn_w_down': nrm(ks[15], (L, D_FF, D_MODEL), D_FF ** -0.5),
        'norm_ffn_post': gain(ks[16], (L, D_MODEL)),
    }


def reference(x, norm_mix_pre, w_in, ret_gn_w, cmp_k_pos, cmp_k_w1, cmp_k_w2,
              cmp_v_pos, cmp_v_w1, cmp_v_w2, w_out, norm_mix_post, norm_ffn_pre,
              ffn_w_up, ffn_conv, ffn_w_down, norm_ffn_post):
    B, S, _ = x.shape
    cos, sin = rope_angles(jnp.arange(S, dtype=jnp.int32), NSA_DK)
    split_at = [int(v) for v in np.cumsum(SPLITS)[:-1]]
    for l in range(DEPTH):
        h = rms_norm(x, norm_mix_pre[l])
        proj = h @ w_in[l]
        (rq, rk, rv, rg, nq, kcm, vcm, ksl, vsl, kwn, vwn, ng) = jnp.split(proj, split_at, axis=-1)
        y_ret = retention(rq.reshape(B, S, RET_HEADS, RET_DK), rk.reshape(B, S, RET_HEADS, RET_DK),
                          rv, rg, ret_gn_w[l], cos, sin)
        kv = lambda a: a.reshape(B, S, NSA_KV_HEADS, NSA_DK)
        y_nsa = nsa(nq.reshape(B, S, NSA_HEADS, NSA_DK), kv(kcm), kv(vcm), kv(ksl), kv(vsl), kv(kwn), kv(vwn), ng,
                    cmp_k_pos[l], cmp_k_w1[l], cmp_k_w2[l], cmp_v_pos[l], cmp_v_w1[l], cmp_v_w2[l], cos, sin)
        mix = jnp.concatenate([y_ret, y_nsa], axis=-1).astype(x.dtype) @ w_out[l]
        x = x + rms_norm(mix, norm_mix_post[l])
        h = rms_norm(x, norm_ffn_pre[l])
        x = x + rms_norm(conv_ffn(h, ffn_w_up[l], ffn_conv[l], ffn_w_down[l]), norm_ffn_post[l])
    return x
```

```python
import numpy as np
import concourse.bass as bass
import concourse.mybir as mybir
from concourse.bass_utils import run_bass_kernel_spmd

F32 = mybir.dt.float32
BF16 = mybir.dt.bfloat16
AF = mybir.ActivationFunctionType
ALU = mybir.AluOpType
AX = mybir.AxisListType


class T:
    __slots__ = ("name", "ap", "w", "r", "sem", "semcnt", "excl")

    def __init__(self, name, ap, excl=False):
        self.name = name
        self.ap = ap
        self.excl = excl
        self.w = []
        self.r = []
        self.sem = None
        self.semcnt = 0

    def __getitem__(self, idx):
        return self.ap[idx]


class Op:
    __slots__ = ("eng", "fn", "deps", "needed", "ordinal", "dma", "group")

    def __init__(self, eng, fn):
        self.eng = eng
        self.fn = fn
        self.deps = []
        self.needed = False
        self.ordinal = None
        self.dma = None
        self.group = None


class Prog:
    ENGS = ("pe", "act", "dve", "pool", "sp")

    def __init__(self, nc, same_engine_sync=True):
        self.nc = nc
        self.ops = {e: [] for e in self.ENGS}
        self.same_engine_sync = same_engine_sync
        self.n_dma_sems = 0
        self.final_events = []

    def _add_deps(self, op, reads, writes):
        deps = []
        for t in reads:
            deps.extend(t.w)
        for t in writes:
            deps.extend(t.w)
            deps.extend(t.r)
        for d in deps:
            if isinstance(d, Op):
                if d is op:
                    continue
                if d.eng == op.eng and op.dma is None and d.dma is None:
                    if op.eng == "pe" or not self.same_engine_sync:
                        continue
                if op.group is not None and d.group == op.group:
                    continue
                d.needed = True
            op.deps.append(d)

    def _commit(self, op, reads, writes):
        ev = op
        for t in reads:
            t.r.append(ev)
        for t in writes:
            if op.group is not None and t.w and all(
                    isinstance(w, Op) and w.group == op.group for w in t.w):
                t.w.append(ev)
            else:
                t.w = [ev]
                t.r = []

    def op(self, eng, fn, reads=(), writes=()):
        o = Op(eng, fn)
        ex = [t for t in reads if t.excl]
        if ex:
            writes = list(writes) + [t for t in ex if t not in writes]
            reads = [t for t in reads if not t.excl]
        self._add_deps(o, reads, writes)
        self.ops[eng].append(o)
        self._commit(o, reads, writes)
        return o

    def dma(self, eng, fn, sbuf_tile, reads=(), writes=(), group=None):
        o = Op(eng, fn)
        o.group = group
        if sbuf_tile.sem is None:
            sbuf_tile.sem = self.n_dma_sems
            self.n_dma_sems += 1
        sbuf_tile.semcnt += 16
        o.dma = (sbuf_tile.sem, sbuf_tile.semcnt)
        self._add_deps(o, reads, writes)
        self.ops[eng].append(o)
        self._commit(o, reads, writes)
        return o

    def emit(self, final_ops=()):
        nc = self.nc
        for o in final_ops:
            if o.dma is None:
                o.needed = True
        for e in self.ENGS:
            n = 0
            for o in self.ops[e]:
                if o.dma is None and o.needed:
                    n += 1
                    o.ordinal = n
        import contextlib
        with contextlib.ExitStack() as st:
            esem = {e: st.enter_context(nc.semaphore("s_" + e)) for e in self.ENGS}
            dsem = [st.enter_context(nc.semaphore("d%d" % i)) for i in range(self.n_dma_sems)]
            block = st.enter_context(nc.Block())

            def ev_key(d):
                if d.dma is not None:
                    return ("d", d.dma[0]), d.dma[1]
                return ("e", d.eng), d.ordinal

            def run(engname, eng):
                waited = {}
                for o in self.ops[engname]:
                    need = {}
                    for d in o.deps:
                        k, c = ev_key(d)
                        if c > need.get(k, 0):
                            need[k] = c
                    for k, c in need.items():
                        if waited.get(k, 0) >= c:
                            continue
                        waited[k] = c
                        sem = dsem[k[1]] if k[0] == "d" else esem[k[1]]
                        eng.wait_ge(sem, c)
                    ins = o.fn(eng)
                    if o.dma is not None:
                        ins.then_inc(dsem[o.dma[0]], 16)
                    elif o.needed:
                        ins.then_inc(esem[engname], 1)
                if engname == "sp":
                    for o in final_ops:
                        k, c = ev_key(o)
                        if waited.get(k, 0) >= c:
                            continue
                        waited[k] = c
                        sem = dsem[k[1]] if k[0] == "d" else esem[k[1]]
                        eng.wait_ge(sem, c)

            @block.tensor
            def _(eng):
                run("pe", eng)

            @block.scalar
            def _(eng):
                run("act", eng)

            @block.vector
            def _(eng):
                run("dve", eng)

            @block.gpsimd
            def _(eng):
                run("pool", eng)

            @block.sync
            def _(eng):
                run("sp", eng)


D = 1024
S = 8192
NB = 2
NT = 2048
TG = 512
NGRP = NT // TG
IN_COLS = 2840
OFF = dict(rq=0, rk=256, rv=512, rg=1024, nq=1536, kcm=2048, vcm=2176, ksl=2304,
           vsl=2432, kwn=2560, vwn=2688, ng=2816)
DFF = 2816
EPS = 1e-6

A_ROPE = [("rq", 0, 1.0), ("rq", 128, 1.0), ("rk", 0, 0.125), ("rk", 128, 0.125),
          ("nq", 0, 0.125), ("nq", 128, 0.125), ("nq", 256, 0.125), ("nq", 384, 0.125),
          ("ksl", 0, 1.0), ("kwn", 0, 1.0)]
A_PLAIN = [("rv", 0), ("rv", 128), ("rv", 256), ("rv", 384),
           ("rg", 0), ("rg", 128), ("rg", 256), ("rg", 384),
           ("kcm", 0), ("vcm", 0), ("vsl", 0), ("vwn", 0), ("ng", 0)]
A_NCH = 2 * len(A_ROPE) + len(A_PLAIN)
A_NOUT = len(A_ROPE) + len(A_PLAIN)


def a_weight_cols():
    cols = []
    for name, o, _ in A_ROPE:
        base = OFF[name] + o
        cols.append(np.arange(base, base + 128))
        sw = np.concatenate([np.arange(base + 32, base + 64), np.arange(base, base + 32),
                             np.arange(base + 96, base + 128), np.arange(base + 64, base + 96)])
        cols.append(sw)
    for name, o in A_PLAIN:
        base = OFF[name] + o
        if name == "ng":
            c = np.full(128, -1)
            c[:24] = np.arange(base, base + 24)
        else:
            c = np.arange(base, base + 128)
        cols.append(c)
    return np.concatenate(cols)


def rope_tables(pos):
    inv = 1.0 / (10000.0 ** (np.arange(0, 64, 2, dtype=np.float32) / 64))
    ang = pos.astype(np.float32)[None, :] * inv[:, None].astype(np.float32)
    c = np.cos(ang).astype(np.float32)
    s = np.sin(ang).astype(np.float32)
    cosF = np.concatenate([c, c, c, c], axis=0)
    sinS = np.concatenate([-s, s, -s, s], axis=0)
    return np.ascontiguousarray(cosF), np.ascontiguousarray(sinS)


def lam(f, *a, **k):
    return lambda e: f(e, *a, **k)


def rms_stats(P, pe_ps, ps_rb, sq, ones_bf, ones_row, rs, rstd, width):
    for c in range(8):
        P.op("pe", lambda e, c=c: e.matmul(pe_ps[0:1, 0:width], lhsT=ones_bf[:, 0:1], rhs=sq[:, c, 0:width],
                                         start=(c == 0), stop=(c == 7)),
             reads=[ones_bf, sq], writes=[pe_ps])
    P.op("act", lambda e: e.activation(out=rs[0:1, 0:width], in_=pe_ps[0:1, 0:width], func=AF.Sqrt,
                                       bias=EPS, scale=1.0 / D), reads=[pe_ps], writes=[rs])
    P.op("dve", lambda e: e.reciprocal(out=rstd[0:1, 0:width], in_=rs[0:1, 0:width]), reads=[rs], writes=[rstd])
    P.op("pe", lambda e: e.matmul(ps_rb[:, 0:width], lhsT=ones_row[0:1, :], rhs=rstd[0:1, 0:width],
                                  start=True, stop=True), reads=[ones_row, rstd], writes=[ps_rb])


def build_A():
    import contextlib
    nc = bass.Bass("TRN2", target_bir_lowering=False)
    xT = nc.dram_tensor("xT", [D, NT], F32, kind="ExternalInput").ap()
    w = nc.dram_tensor("w", [D, A_NCH * 128], F32, kind="ExternalInput").ap()
    cosd = nc.dram_tensor("cosF", [128, NT], F32, kind="ExternalInput").ap()
    sind = nc.dram_tensor("sinS", [128, NT], F32, kind="ExternalInput").ap()
    nwd = nc.dram_tensor("nw", [128, 8], F32, kind="ExternalInput").ap()
    out = nc.dram_tensor("pT", [A_NOUT * 128, NT], F32, kind="ExternalOutput").ap()
    xTv = xT.rearrange("(c p) t -> p c t", p=128)
    wv = w.rearrange("(c p) n -> p c n", p=128)
    with contextlib.ExitStack() as st:
        def sb(name, shape, dt):
            return T(name, st.enter_context(nc.sbuf_tensor("sb_" + name, shape, dt)))

        def pp(name):
            return T(name, st.enter_context(nc.psum_tensor("pp_" + name, [128, 512], F32)), excl=True)
        W = sb("W", [128, 8, A_NCH * 128], BF16)
        nw = sb("nw", [128, 8], F32)
        ones_bf = sb("ones_bf", [128, 1], BF16)
        ones_row = sb("ones_row", [1, 128], F32)
        cos = sb("cos", [128, NT], F32)
        sin = sb("sin", [128, NT], F32)
        xg = [sb("xg%d" % i, [128, 8, TG], F32) for i in range(2)]
        sq = sb("sq", [128, 8, TG], BF16)
        hT = [sb("hT%d" % i, [128, 8, TG], BF16) for i in range(2)]
        rs = sb("rs", [1, TG], F32)
        rstd = sb("rstd", [1, TG], F32)
        stage = [sb("stage%d" % i, [128, TG], F32) for i in range(4)]
        t1 = [sb("t1_%d" % i, [128, TG], F32) for i in range(2)]
        t2 = [sb("t2_%d" % i, [128, TG], F32) for i in range(2)]
        ps_stat = pp("ps_stat")
        ps_rb = pp("ps_rb")
        ps = [pp("ps%d" % i) for i in range(6)]
        P = Prog(nc)
        P.dma("sp", lambda e: e.dma_start(out=nw[:], in_=nwd[:, :]), nw, writes=[nw])
        for g in range(min(2, NGRP)):
            P.dma("sp", lambda e, g=g: e.dma_start(out=xg[g][:], in_=xTv[:, :, g * TG:(g + 1) * TG]),
                  xg[g], writes=[xg[g]])
        for c in range(8):
            P.dma("pool", lambda e, c=c: e.dma_start(out=W[:, c, :], in_=wv[:, c, :]), W, writes=[W], group="W")
        P.dma("sp", lambda e: e.dma_start(out=cos[:], in_=cosd[:, :]), cos, writes=[cos])
        P.dma("sp", lambda e: e.dma_start(out=sin[:], in_=sind[:, :]), sin, writes=[sin])
        P.op("dve", lambda e: e.memset(ones_bf[:], 1.0), writes=[ones_bf])
        P.op("dve", lambda e: e.memset(ones_row[:], 1.0), writes=[ones_row])
        finals = []
        psi = 0
        sti = 0
        for g in range(NGRP):
            x_ = xg[g % 2]
            h_ = hT[g % 2]
            tsl = slice(g * TG, (g + 1) * TG)
            P.op("act", lambda e, x_=x_: e.activation(out=sq[:], in_=x_[:], func=AF.Square), reads=[x_], writes=[sq])
            rms_stats(P, ps_stat, ps_rb, sq, ones_bf, ones_row, rs, rstd, TG)
            for c in range(8):
                P.op("dve", lambda e, c=c, x_=x_, h_=h_: e.scalar_tensor_tensor(
                    out=h_[:, c, :], in0=x_[:, c, :], scalar=nw[:, c:c + 1], in1=ps_rb[:],
                    op0=ALU.mult, op1=ALU.mult), reads=[x_, nw, ps_rb], writes=[h_])
            if g + 2 < NGRP:
                P.dma("sp", lambda e, g=g, x_=x_: e.dma_start(out=x_[:], in_=xTv[:, :, (g + 2) * TG:(g + 3) * TG]),
                      x_, writes=[x_])

            def proj(j, pst):
                for c in range(8):
                    P.op("pe", lambda e, c=c, j=j, pst=pst, h_=h_: e.matmul(
                        pst[:], lhsT=W[:, c, j * 128:(j + 1) * 128], rhs=h_[:, c, :],
                        start=(c == 0), stop=(c == 7)), reads=[W, h_], writes=[pst])
            for r, (name, o, scale) in enumerate(A_ROPE):
                pa = ps[psi % 6]; psi += 1
                pb = ps[psi % 6]; psi += 1
                proj(2 * r, pa)
                proj(2 * r + 1, pb)
                ta = t1[r % 2]; tb = t2[r % 2]
                stg = stage[sti % 4]; sti += 1
                P.op("dve", lambda e, pa=pa, ta=ta, scale=scale, tsl=tsl: e.scalar_tensor_tensor(
                    out=ta[:], in0=pa[:], scalar=scale, in1=cos[:, tsl], op0=ALU.mult, op1=ALU.mult),
                    reads=[pa, cos], writes=[ta])
                P.op("dve", lambda e, pb=pb, tb=tb, scale=scale, tsl=tsl: e.scalar_tensor_tensor(
                    out=tb[:], in0=pb[:], scalar=scale, in1=sin[:, tsl], op0=ALU.mult, op1=ALU.mult),
                    reads=[pb, sin], writes=[tb])
                P.op("pool", lambda e, ta=ta, tb=tb, stg=stg: e.tensor_tensor(
                    out=stg[:], in0=ta[:], in1=tb[:], op=ALU.add), reads=[ta, tb], writes=[stg])
                finals.append(P.dma("sp", lambda e, r=r, stg=stg, tsl=tsl: e.dma_start(
                    out=out[r * 128:(r + 1) * 128, tsl], in_=stg[:]), stg, reads=[stg]))
            for q, (name, o) in enumerate(A_PLAIN):
                pa = ps[psi % 6]; psi += 1
                proj(2 * len(A_ROPE) + q, pa)
                stg = stage[sti % 4]; sti += 1
                P.op("act", lambda e, pa=pa, stg=stg: e.copy(out=stg[:], in_=pa[:]), reads=[pa], writes=[stg])
                ro = (len(A_ROPE) + q) * 128
                finals.append(P.dma("sp", lambda e, ro=ro, stg=stg, tsl=tsl: e.dma_start(
                    out=out[ro:ro + 128, tsl], in_=stg[:]), stg, reads=[stg]))
        P.emit(final_ops=finals[-8:])
    return nc


def run_A(x, w_in_l, nw_l):
    cols = a_weight_cols()
    wz = np.concatenate([w_in_l, np.zeros((D, 1), np.float32)], axis=1)
    wext = np.ascontiguousarray(wz[:, cols])
    nw = np.ascontiguousarray(nw_l.reshape(8, 128).T)
    in_maps = []
    for c in range(8):
        b, q = divmod(c, 4)
        pos = np.arange(q * NT, (q + 1) * NT)
        cosF, sinS = rope_tables(pos)
        in_maps.append({"xT": np.ascontiguousarray(x[b, q * NT:(q + 1) * NT, :].T), "w": wext,
                        "cosF": cosF, "sinS": sinS, "nw": nw})
    nc = build_A()
    res = run_bass_kernel_spmd(nc, in_maps, core_ids=list(range(8)))
    return [r["pT"] for r in res.results]


CG = 256
NCC = NT + 2
GELU_TANH_NATIVE = True


def build_C(dbg=0):
    import contextlib
    nc = bass.Bass("TRN2", target_bir_lowering=False)
    xT = nc.dram_tensor("xT", [D, NCC], F32, kind="ExternalInput").ap()
    yT = nc.dram_tensor("yT", [D, NCC], F32, kind="ExternalInput").ap()
    wo = nc.dram_tensor("wo", [D, D], F32, kind="ExternalInput").ap()
    wu = nc.dram_tensor("wu", [D, 2 * DFF], F32, kind="ExternalInput").ap()
    wd = nc.dram_tensor("wd", [DFF, D], F32, kind="ExternalInput").ap()
    cwd = nc.dram_tensor("cw", [128, 44, 3], F32, kind="ExternalInput").ap()
    nwd = nc.dram_tensor("nw", [128, 3, 8], F32, kind="ExternalInput").ap()
    out = nc.dram_tensor("oT", [D, NT], F32, kind="ExternalOutput").ap()
    xTv = xT.rearrange("(c p) t -> p c t", p=128)
    yTv = yT.rearrange("(c p) t -> p c t", p=128)
    oTv = out.rearrange("(c p) t -> p c t", p=128)
    wov = wo.rearrange("(c p) n -> p c n", p=128)
    wuv = wu.rearrange("(c p) n -> p c n", p=128)
    wdv = wd.rearrange("(c p) n -> p c n", p=128)
    with contextlib.ExitStack() as st:
        def sb(name, shape, dt):
            return T(name, st.enter_context(nc.sbuf_tensor("sb_" + name, shape, dt)))

        def pp(name):
            return T(name, st.enter_context(nc.psum_tensor("pp_" + name, [128, 512], F32)), excl=True)
        Wo = sb("Wo", [128, 8, D], BF16)
        Wu = sb("Wu", [128, 8, 2 * DFF], BF16)
        Wd = sb("Wd", [128, 22, D], BF16)
        cw = sb("cw", [128, 44, 3], F32)
        nw = sb("nw", [128, 3, 8], F32)
        ones_bf = sb("ones_bf", [128, 1], BF16)
        ones_row = sb("ones_row", [1, 128], F32)
        xg = sb("xg", [128, 8, CG], F32)
        bfb = sb("bfb", [128, 8, CG], BF16)
        mixf = sb("mixf", [128, 8, CG], F32)
        sq = sb("sq", [128, 8, CG], BF16)
        gT = sb("gT", [128, 22, CG], BF16)
        carry = sb("carry", [128, 44, 2], F32)
        ug = [sb("ug%d" % i, [128, CG + 2], F32) for i in range(2)]
        uv = [sb("uv%d" % i, [128, CG + 2], F32) for i in range(2)]
        ag = [sb("ag%d" % i, [128, CG], F32) for i in range(2)]
        av = [sb("av%d" % i, [128, CG], F32) for i in range(2)]
        gg = [sb("gg%d" % i, [128, CG], F32) for i in range(2)]
        rs = sb("rs", [1, CG], F32)
        rstd = sb("rstd", [1, CG], F32)
        ps_stat = pp("ps_stat")
        ps_rb = pp("ps_rb")
        ps = [pp("ps%d" % i) for i in range(6)]
        P = Prog(nc)
        P.dma("sp", lambda e: e.dma_start(out=nw[:], in_=nwd[:, :, :]), nw, writes=[nw])
        P.dma("sp", lambda e: e.dma_start(out=cw[:], in_=cwd[:, :, :]), cw, writes=[cw])
        for c in range(8):
            P.dma("pool", lambda e, c=c: e.dma_start(out=Wo[:, c, :], in_=wov[:, c, :]), Wo, writes=[Wo], group="Wo")
        for c in range(8):
            P.dma("pool", lambda e, c=c: e.dma_start(out=Wu[:, c, :], in_=wuv[:, c, :]), Wu, writes=[Wu], group="Wu")
        for c in range(22):
            P.dma("pool", lambda e, c=c: e.dma_start(out=Wd[:, c, :], in_=wdv[:, c, :]), Wd, writes=[Wd], group="Wd")
        P.op("dve", lambda e: e.memset(ones_bf[:], 1.0), writes=[ones_bf])
        P.op("dve", lambda e: e.memset(ones_row[:], 1.0), writes=[ones_row])
        P.op("dve", lambda e: e.memset(carry[:], 0.0), writes=[carry])
        groups = [(0, 2)] + [(2 + g * CG, CG) for g in range(NT // CG)]
        if dbg & 1:
            groups = groups[1:3]
        finals = []
        psi = [0]

        def nps():
            p = ps[psi[0] % 6]
            psi[0] += 1
            return p

        def norm_apply(site, wdt):
            rms_stats(P, ps_stat, ps_rb, sq, ones_bf, ones_row, rs, rstd, wdt)
            for c in range(8):
                P.op("dve", lambda e, c=c: e.tensor_tensor(out=mixf[:, c, 0:wdt], in0=mixf[:, c, 0:wdt],
                                                            in1=ps_rb[:, 0:wdt], op=ALU.mult),
                     reads=[mixf, ps_rb], writes=[mixf])
                P.op("pool", lambda e, c=c: e.tensor_scalar(
                    out=mixf[:, c, 0:wdt], in0=mixf[:, c, 0:wdt], scalar1=nw[:, site, c:c + 1], scalar2=None,
                    op0=ALU.mult), reads=[mixf, nw], writes=[mixf])
                P.op("pool", lambda e, c=c: e.tensor_tensor(
                    out=xg[:, c, 0:wdt], in0=xg[:, c, 0:wdt], in1=mixf[:, c, 0:wdt], op=ALU.add),
                    reads=[mixf, xg], writes=[xg])

        for gi, (s0, wdt) in enumerate(groups):
            halo = (s0 == 0)
            P.dma("sp", lambda e, s0=s0, wdt=wdt: e.dma_start(out=xg[:, :, 0:wdt], in_=xTv[:, :, s0:s0 + wdt]),
                  xg, writes=[xg])
            P.dma("pool", lambda e, s0=s0, wdt=wdt: e.dma_start(out=bfb[:, :, 0:wdt], in_=yTv[:, :, s0:s0 + wdt]),
                  bfb, writes=[bfb])
            for oc in range(8):
                p_ = nps()
                for k in range(8):
                    P.op("pe", lambda e, k=k, oc=oc, p_=p_, wdt=wdt: e.matmul(
                        p_[:, 0:wdt], lhsT=Wo[:, k, oc * 128:(oc + 1) * 128], rhs=bfb[:, k, 0:wdt],
                        start=(k == 0), stop=(k == 7)), reads=[Wo, bfb], writes=[p_])
                P.op("dve", lambda e, oc=oc, p_=p_, wdt=wdt: e.tensor_copy(out=mixf[:, oc, 0:wdt], in_=p_[:, 0:wdt]),
                     reads=[p_], writes=[mixf])
                P.op("act", lambda e, oc=oc, wdt=wdt: e.activation(out=sq[:, oc, 0:wdt], in_=mixf[:, oc, 0:wdt], func=AF.Square),
                     reads=[mixf], writes=[sq])
            norm_apply(0, wdt)
            P.op("act", lambda e, wdt=wdt: e.activation(out=sq[:, :, 0:wdt], in_=xg[:, :, 0:wdt], func=AF.Square),
                 reads=[xg], writes=[sq])
            rms_stats(P, ps_stat, ps_rb, sq, ones_bf, ones_row, rs, rstd, wdt)
            for c in range(8):
                P.op("dve", lambda e, c=c, wdt=wdt: e.scalar_tensor_tensor(
                    out=bfb[:, c, 0:wdt], in0=xg[:, c, 0:wdt], scalar=nw[:, 1, c:c + 1], in1=ps_rb[:, 0:wdt],
                    op0=ALU.mult, op1=ALU.mult), reads=[xg, nw, ps_rb], writes=[bfb])
            for fc in range(22):
                pg = nps()
                pv = nps()
                for (p_, col) in ((pg, fc), (pv, 22 + fc)):
                    for k in range(8):
                        P.op("pe", lambda e, k=k, col=col, p_=p_, wdt=wdt: e.matmul(
                            p_[:, 0:wdt], lhsT=Wu[:, k, col * 128:(col + 1) * 128], rhs=bfb[:, k, 0:wdt],
                            start=(k == 0), stop=(k == 7)), reads=[Wu, bfb], writes=[p_])
                ug_, uv_, ag_, av_, gg_ = ug[fc % 2], uv[fc % 2], ag[fc % 2], av[fc % 2], gg[fc % 2]
                for (p_, u_, col) in ((pg, ug_, fc), (pv, uv_, 22 + fc)):
                    P.op("act", lambda e, p_=p_, u_=u_, wdt=wdt: e.copy(out=u_[:, 2:2 + wdt], in_=p_[:, 0:wdt]),
                         reads=[p_], writes=[u_])
                    if not halo:
                        P.op("pool", lambda e, u_=u_, col=col: e.tensor_copy(out=u_[:, 0:2], in_=carry[:, col, :]),
                             reads=[carry, u_], writes=[u_])
                    P.op("pool", lambda e, u_=u_, col=col, wdt=wdt: e.tensor_copy(out=carry[:, col, :], in_=u_[:, wdt:wdt + 2]),
                         reads=[u_], writes=[carry])
                if halo:
                    continue
                for (eng, u_, a_, col) in (("dve", ug_, ag_, fc), ("dve", uv_, av_, 22 + fc)):
                    P.op("act", lambda e, u_=u_, a_=a_, col=col, wdt=wdt: e.activation(
                        out=a_[:, 0:wdt], in_=u_[:, 0:wdt], func=AF.Identity, scale=cw[:, col, 0:1]),
                        reads=[u_, cw], writes=[a_])
                    for j in (1, 2):
                        P.op(eng, lambda e, u_=u_, a_=a_, col=col, wdt=wdt, j=j: e.scalar_tensor_tensor(
                            out=a_[:, 0:wdt], in0=u_[:, j:j + wdt], scalar=cw[:, col, j:j + 1], in1=a_[:, 0:wdt],
                            op0=ALU.mult, op1=ALU.add), reads=[u_, cw, a_], writes=[a_])
                if GELU_TANH_NATIVE and not (dbg & 2):
                    P.op("act", lambda e, a_=ag_, g_=gg_, wdt=wdt: e.activation(out=g_[:, 0:wdt], in_=a_[:, 0:wdt],
                                                                                func=AF.Gelu_apprx_tanh),
                         reads=[ag_], writes=[gg_])
                else:
                    gelu_tanh(P, ag_, gg_, wdt)
                P.op("dve", lambda e, g_=gg_, a_=av_, fc=fc, wdt=wdt: e.tensor_tensor(
                    out=gT[:, fc, 0:wdt], in0=g_[:, 0:wdt], in1=a_[:, 0:wdt], op=ALU.mult),
                    reads=[gg_, av_], writes=[gT])
            if halo:
                continue
            for oc in range(8):
                p_ = nps()
                for f in range(22):
                    P.op("pe", lambda e, f=f, oc=oc, p_=p_, wdt=wdt: e.matmul(
                        p_[:, 0:wdt], lhsT=Wd[:, f, oc * 128:(oc + 1) * 128], rhs=gT[:, f, 0:wdt],
                        start=(f == 0), stop=(f == 21)), reads=[Wd, gT], writes=[p_])
                P.op("dve", lambda e, oc=oc, p_=p_, wdt=wdt: e.tensor_copy(out=mixf[:, oc, 0:wdt], in_=p_[:, 0:wdt]),
                     reads=[p_], writes=[mixf])
                P.op("act", lambda e, oc=oc, wdt=wdt: e.activation(out=sq[:, oc, 0:wdt], in_=mixf[:, oc, 0:wdt], func=AF.Square),
                     reads=[mixf], writes=[sq])
            norm_apply(2, wdt)
            finals.append(P.dma("sp", lambda e, s0=s0, wdt=wdt: e.dma_start(out=oTv[:, :, s0 - 2:s0 - 2 + wdt], in_=xg[:, :, 0:wdt]),
                                xg, reads=[xg]))
        P.emit(final_ops=finals[-2:])
    return nc


def gelu_tanh(P, a_, g_, wdt):
    P.op("act", lambda e: e.activation(out=g_[:, 0:wdt], in_=a_[:, 0:wdt], func=AF.Square), reads=[a_], writes=[g_])
    P.op("dve", lambda e: e.tensor_scalar(out=g_[:, 0:wdt], in0=g_[:, 0:wdt], scalar1=0.044715, scalar2=1.0,
                                          op0=ALU.mult, op1=ALU.add), reads=[g_], writes=[g_])
    P.op("dve", lambda e: e.tensor_tensor(out=g_[:, 0:wdt], in0=g_[:, 0:wdt], in1=a_[:, 0:wdt], op=ALU.mult),
         reads=[g_, a_], writes=[g_])
    P.op("act", lambda e: e.activation(out=g_[:, 0:wdt], in_=g_[:, 0:wdt], func=AF.Sigmoid, scale=1.5957691216057308),
         reads=[g_], writes=[g_])
    P.op("dve", lambda e: e.tensor_tensor(out=g_[:, 0:wdt], in0=g_[:, 0:wdt], in1=a_[:, 0:wdt], op=ALU.mult),
         reads=[g_, a_], writes=[g_])


def run_C(x, ymix, l, inputs, dbg=0):
    nw = np.stack([inputs["norm_mix_post"][l], inputs["norm_ffn_pre"][l], inputs["norm_ffn_post"][l]])
    nw = np.ascontiguousarray(nw.reshape(3, 8, 128).transpose(2, 0, 1))
    cw = np.ascontiguousarray(inputs["ffn_conv"][l].reshape(3, 44, 128).transpose(2, 1, 0))
    in_maps = []
    for c in range(8):
        b, q = divmod(c, 4)
        t0 = q * NT
        xs = np.zeros((NCC, D), np.float32)
        ys = np.zeros((NCC, D), np.float32)
        xs[2:] = x[b, t0:t0 + NT]
        ys[2:] = ymix[b, t0:t0 + NT]
        if t0 > 0:
            xs[:2] = x[b, t0 - 2:t0]
            ys[:2] = ymix[b, t0 - 2:t0]
        in_maps.append({"xT": np.ascontiguousarray(xs.T), "yT": np.ascontiguousarray(ys.T),
                        "wo": inputs["w_out"][l], "wu": inputs["ffn_w_up"][l], "wd": inputs["ffn_w_down"][l],
                        "cw": cw, "nw": nw})
    nc = build_C(dbg)
    res = run_bass_kernel_spmd(nc, in_maps, core_ids=list(range(8)))
    xo = np.empty_like(x)
    for c in range(8):
        b, q = divmod(c, 4)
        xo[b, q * NT:(q + 1) * NT] = res.results[c]["oT"].T
    return xo


NOWN = 16
SS = 65 * 128
RET_GAMMA = [1.0 - 2.0 ** (-5.0 - h) for h in range(4)]
NEGB = -30000.0


def ret_consts():
    i = np.arange(128, dtype=np.float64)
    decT = np.zeros((128, 4, 128), np.float32)
    zk = np.zeros((128, 256), np.float32)
    xiT = np.zeros((64, 4, 128), np.float32)
    cdec = np.zeros((64, 512), np.float32)
    for h, gm in enumerate(RET_GAMMA):
        lg = np.log(np.float32(gm)).astype(np.float32).astype(np.float64)
        diff = i[None, :] - i[:, None]
        decT[:, h, :] = np.where(diff >= 0, np.exp(lg * np.maximum(diff, 0)), 0.0)
        zk[:, h * 64:(h + 1) * 64] = np.exp(lg * (127.0 - i))[:, None]
        xiT[:, h, :] = np.exp(lg * (i + 1.0))[None, :]
        cdec[:, h * 128:(h + 1) * 128] = np.exp(lg * 128.0)
    return decT, zk, xiT, cdec


def build_B(parts=("ret", "nsa")):
    import contextlib
    nc = bass.Bass("TRN2", target_bir_lowering=False)
    din = {}

    def inp(name, shape):
        din[name] = nc.dram_tensor(name, shape, F32, kind="ExternalInput").ap()
        return din[name]
    rqT_d = inp("rqT", [64, 4, NT]); rkT_d = inp("rkT", [64, 4, NT])
    rk_d = inp("rk_tok", [SS, 256]); rv_d = inp("rv_tok", [SS, 512]); rg_d = inp("rg_own", [NT, 512])
    decT_d = inp("decT", [128, 4, 128]); zk_d = inp("zk", [128, 256]); xiT_d = inp("xiT", [64, 4, 128])
    cdec_d = inp("cdec", [64, 512]); gnw_d = inp("gnw", [128, 512])
    yret_d = nc.dram_tensor("yret", [NT, 512], F32, kind="ExternalOutput").ap()
    with contextlib.ExitStack() as st:
        def sb(name, shape, dt):
            return T(name, st.enter_context(nc.sbuf_tensor("sb_" + name, shape, dt)))

        def pp(name):
            return T(name, st.enter_context(nc.psum_tensor("pp_" + name, [128, 512], F32)), excl=True)
        P = Prog(nc)
        bank = [pp("b%d" % i) for i in range(8)]
        finals = []
        if "ret" in parts:
            rqTb = [sb("rqT%d" % i, [64, 4, 128], BF16) for i in range(2)]
            rkTb = [sb("rkT%d" % i, [64, 4, 128], BF16) for i in range(2)]
            decT = sb("decT", [128, 4, 128], F32); zk = sb("zk", [128, 256], F32)
            xiT = sb("xiT", [64, 4, 128], F32); cdec = sb("cdec", [64, 512], F32); gnw = sb("gnw", [128, 512], F32)
            kt = [sb("kt%d" % i, [128, 256], BF16) for i in range(2)]
            vt = [sb("vt%d" % i, [128, 512], BF16) for i in range(2)]
            kz = sb("kz", [128, 256], BF16)
            Sst = sb("Sst", [64, 512], F32); Sbf = sb("Sbf", [64, 512], BF16)
            at_bf = sb("at_bf", [128, 512], BF16); qxi = sb("qxi", [64, 4, 128], BF16)
            o_sb = sb("o_sb", [128, 512], F32); osq = sb("osq", [128, 512], F32)
            yn = sb("yn", [128, 512], F32); rg = sb("rg", [128, 512], F32); sg = sb("sg", [128, 512], F32)
            sm = sb("sm", [128, 4], F32); ssq = sb("ssq", [128, 4], F32); mean = sb("mean", [128, 4], F32)
            msq = sb("msq", [128, 4], F32); var = sb("var", [128, 4], F32); rstd = sb("rstd4", [128, 4], F32)
            for (t_, d_) in ((decT, decT_d), (xiT, xiT_d)):
                P.dma("sp", lambda e, t_=t_, d_=d_: e.dma_start(out=t_[:], in_=d_[:, :, :]), t_, writes=[t_])
            for (t_, d_) in ((zk, zk_d), (cdec, cdec_d), (gnw, gnw_d)):
                P.dma("sp", lambda e, t_=t_, d_=d_: e.dma_start(out=t_[:], in_=d_[:, :]), t_, writes=[t_])
            P.op("dve", lambda e: e.memset(Sst[:], 0.0), writes=[Sst])
            P.op("dve", lambda e: e.memset(Sbf[:], 0.0), writes=[Sbf])
            psS, psO, psK = bank[0], bank[2], bank[4]

            for n in range(65):
                k_ = kt[n % 2]; v_ = vt[n % 2]
                P.dma("pool", lambda e, n=n, k_=k_: e.dma_start(out=k_[:], in_=rk_d[n * 128:(n + 1) * 128, :]), k_, writes=[k_])
                P.dma("pool", lambda e, n=n, v_=v_: e.dma_start(out=v_[:], in_=rv_d[n * 128:(n + 1) * 128, :]), v_, writes=[v_])
                if n % 4 == 0 and n >= 4:
                    i = n // 4 - 1
                    tsl = slice(i * 128, (i + 1) * 128)
                    rqT = rqTb[i % 2]; rkT = rkTb[i % 2]
                    P.dma("pool", lambda e, rqT=rqT, tsl=tsl: e.dma_start(out=rqT[:], in_=rqT_d[:, :, tsl]), rqT, writes=[rqT])
                    P.dma("pool", lambda e, rkT=rkT, tsl=tsl: e.dma_start(out=rkT[:], in_=rkT_d[:, :, tsl]), rkT, writes=[rkT])
                    for h in range(4):
                        P.op("pe", lambda e, h=h, rqT=rqT, rkT=rkT: e.matmul(psS[:, h * 128:(h + 1) * 128], lhsT=rkT[:, h, :],
                                                                    rhs=rqT[:, h, :], start=True, stop=True),
                             reads=[rkT, rqT], writes=[psS])
                    P.op("dve", lambda e: e.tensor_tensor(out=at_bf[:], in0=psS[:], in1=decT[:].rearrange("p h i -> p (h i)"),
                                                          op=ALU.mult), reads=[psS, decT], writes=[at_bf])
                    P.op("pool", lambda e, rqT=rqT: e.tensor_tensor(out=qxi[:], in0=rqT[:], in1=xiT[:], op=ALU.mult),
                         reads=[rqT, xiT], writes=[qxi])
                    for h in range(4):
                        hs = slice(h * 128, (h + 1) * 128)
                        P.op("pe", lambda e, hs=hs, v_=v_: e.matmul(psO[:, hs], lhsT=at_bf[:, hs], rhs=v_[:, hs],
                                                                    start=True, stop=False), reads=[at_bf, v_], writes=[psO])
                        P.op("pe", lambda e, hs=hs, h=h: e.matmul(psO[:, hs], lhsT=qxi[:, h, :], rhs=Sbf[:, hs],
                                                                  start=False, stop=True), reads=[qxi, Sbf], writes=[psO])
                    P.op("act", lambda e: e.copy(out=o_sb[:], in_=psO[:]), reads=[psO], writes=[o_sb])
                    P.op("dve", lambda e: e.reduce_sum(out=sm[:], in_=o_sb[:].rearrange("p (h e) -> p h e", h=4), axis=AX.X),
                         reads=[o_sb], writes=[sm])
                    P.op("act", lambda e: e.activation(out=osq[:], in_=o_sb[:], func=AF.Square), reads=[o_sb], writes=[osq])
                    P.op("dve", lambda e: e.reduce_sum(out=ssq[:], in_=osq[:].rearrange("p (h e) -> p h e", h=4), axis=AX.X),
                         reads=[osq], writes=[ssq])
                    P.op("dve", lambda e: e.tensor_scalar(out=mean[:], in0=sm[:], scalar1=1.0 / 128, scalar2=None, op0=ALU.mult),
                         reads=[sm], writes=[mean])
                    P.op("dve", lambda e: e.tensor_tensor(out=msq[:], in0=mean[:], in1=mean[:], op=ALU.mult),
                         reads=[mean], writes=[msq])
                    P.op("dve", lambda e: e.scalar_tensor_tensor(out=var[:], in0=ssq[:], scalar=1.0 / 128, in1=msq[:],
                                                                 op0=ALU.mult, op1=ALU.subtract), reads=[ssq, msq], writes=[var])
                    P.op("act", lambda e: e.activation(out=var[:], in_=var[:], func=AF.Sqrt, bias=1e-5, scale=1.0),
                         reads=[var], writes=[var])
                    P.op("dve", lambda e: e.reciprocal(out=rstd[:], in_=var[:]), reads=[var], writes=[rstd])
                    for h in range(4):
                        hs = slice(h * 128, (h + 1) * 128)
                        P.op("dve", lambda e, h=h, hs=hs: e.tensor_scalar(out=yn[:, hs], in0=o_sb[:, hs], scalar1=mean[:, h:h + 1],
                                                                           scalar2=rstd[:, h:h + 1], op0=ALU.subtract, op1=ALU.mult),
                             reads=[o_sb, mean, rstd], writes=[yn])
                    P.dma("sp", lambda e, tsl=tsl: e.dma_start(out=rg[:], in_=rg_d[tsl, :]), rg, writes=[rg])
                    P.op("act", lambda e: e.activation(out=sg[:], in_=rg[:], func=AF.Silu), reads=[rg], writes=[sg])
                    P.op("pool", lambda e: e.tensor_tensor(out=yn[:], in0=yn[:], in1=gnw[:], op=ALU.mult), reads=[yn, gnw], writes=[yn])
                    P.op("pool", lambda e: e.tensor_tensor(out=yn[:], in0=yn[:], in1=sg[:], op=ALU.mult), reads=[yn, sg], writes=[yn])
                    finals.append(P.dma("sp", lambda e, tsl=tsl: e.dma_start(out=yret_d[tsl, :], in_=yn[:]), yn, reads=[yn]))
                if n < 64:
                    P.op("dve", lambda e, k_=k_: e.tensor_tensor(out=kz[:], in0=k_[:], in1=zk[:], op=ALU.mult),
                         reads=[k_, zk], writes=[kz])
                    for h in range(4):
                        P.op("pe", lambda e, h=h, v_=v_: e.matmul(psK[0:64, h * 128:(h + 1) * 128], lhsT=kz[:, h * 64:(h + 1) * 64],
                                                                  rhs=v_[:, h * 128:(h + 1) * 128], start=True, stop=True),
                             reads=[kz, v_], writes=[psK])
                    P.op("dve", lambda e: e.tensor_tensor(out=Sst[:], in0=Sst[:], in1=cdec[:], op=ALU.mult), reads=[Sst, cdec], writes=[Sst])
                    P.op("dve", lambda e: e.tensor_tensor(out=Sst[:], in0=Sst[:], in1=psK[0:64, :], op=ALU.add), reads=[Sst, psK], writes=[Sst])
                    if (n + 1) % 4 == 0:
                        P.op("act", lambda e: e.copy(out=Sbf[:], in_=Sst[:]), reads=[Sst], writes=[Sbf])
        if "nsa" in parts:
            build_nsa(nc, P, sb, bank, finals, inp)
        P.emit(final_ops=finals[-3:])
    return nc


def gather_pT(pT_list):
    return [np.concatenate(pT_list[b * 4:(b + 1) * 4], axis=1) for b in range(NB)]


def stream_tok(a_tok, pad):
    out = np.zeros((SS, a_tok.shape[1]), np.float32)
    n = min(S, SS - pad * 128)
    out[pad * 128:pad * 128 + n] = a_tok[:n]
    return out


def run_B(pT_list, l, inputs, parts=("ret", "nsa")):
    pf = gather_pT(pT_list)
    decT, zk, xiT, cdec = ret_consts()
    gnw = np.ascontiguousarray(np.broadcast_to(inputs["ret_gn_w"][l][None, :], (128, 512))).astype(np.float32)
    in_maps = []
    owns = []
    for c in range(8):
        b, r = divmod(c, 4)
        pad = 4 - r
        tok = (np.arange(NOWN)[:, None] * 4 + r) * 128 + np.arange(128)[None, :]
        tok = tok.reshape(-1)
        owns.append((b, tok))
        f = pf[b]
        m = {}
        m["rqT"] = np.ascontiguousarray(f[0:256][:, tok].reshape(4, 64, NT).transpose(1, 0, 2))
        m["rkT"] = np.ascontiguousarray(f[256:512][:, tok].reshape(4, 64, NT).transpose(1, 0, 2))
        m["rk_tok"] = stream_tok(f[256:512].T, pad)
        m["rv_tok"] = stream_tok(f[1280:1792].T, pad)
        m["rg_own"] = np.ascontiguousarray(f[1792:2304][:, tok].T)
        m.update(decT=decT, zk=zk, xiT=xiT, cdec=cdec, gnw=gnw)
        if "nsa" in parts:
            m2, _ = nsa_host_inputs(f, b, r, l, inputs)
            m.update(m2)
        in_maps.append(m)
    nc = build_B(parts)
    res = run_bass_kernel_spmd(nc, in_maps, core_ids=list(range(8)))
    ymix = np.zeros((NB, S, D), np.float32)
    for c in range(8):
        b, tok = owns[c]
        if "ret" in parts:
            ymix[b, tok, 0:512] = res.results[c]["yret"]
        if "nsa" in parts:
            yn_ = res.results[c]["ynsa"]
            ymix[b, tok, 512:1024] = yn_.transpose(2, 1, 0).reshape(NT, 512)
    return ymix


def build_nsa(nc, P, sb, bank, finals, inp):
    qT_d = inp("qT", [64, 8, NT]); gate_d = inp("gate", [1, 3, 8, NT])
    ks_d = inp("ksT", [66, 2, SS]); kw_d = inp("kwT", [66, 2, SS])
    vs_d = inp("vs", [SS, 2, 65]); vw_d = inp("vw", [SS, 2, 65])
    xs_d = {"k": inp("xsK", [128, 2, S + 16]), "v": inp("xsV", [128, 2, S + 16])}
    w1_d = {"k": inp("w1k", [128, 16, 256]), "v": inp("w1v", [128, 16, 256])}
    pos_d = {"k": inp("posk", [128, 16]), "v": inp("posv", [128, 16])}
    w2k_d = inp("w2k", [128, 2, 64]); w2ks_d = inp("w2ks", [128, 2, 64]); w2v_d = inp("w2v", [128, 2, 64])
    cosC_d = inp("cosC", [64, 512]); sinC_d = inp("sinC", [64, 512])
    pa_d = inp("pa", [128, 16, 33]); pm_d = inp("pm", [128, 16, 33])
    scm_d = inp("sc_mul", [128, 16, 128]); sca_d = inp("sc_add", [128, 16, 128])
    E_d = inp("Estream", [128, 65, 128]); cmpb_d = inp("cmpb", [128, 16, 2, 128])
    caus_d = inp("causT4", [128, 512]); low_d = inp("lowT4", [128, 512]); ident_d = inp("ident", [128, 128])
    ynsa_d = nc.dram_tensor("ynsa", [64, 8, NT], F32, kind="ExternalOutput").ap()

    def load(name, d_ap, shape, dt, eng):
        t_ = sb(name, shape, dt)
        nd = len(shape)
        P.dma(eng, lambda e: e.dma_start(out=t_[:], in_=d_ap[tuple([slice(None)] * nd)]), t_, writes=[t_])
        return t_
    w1 = {kv: load("w1" + kv, w1_d[kv], [128, 16, 256], BF16, "pool") for kv in "kv"}
    posf = {kv: load("pos" + kv, pos_d[kv], [128, 16], BF16, "pool") for kv in "kv"}
    w2k = load("w2k", w2k_d, [128, 2, 64], BF16, "pool"); w2ks = load("w2ks", w2ks_d, [128, 2, 64], BF16, "pool")
    w2v = load("w2v", w2v_d, [128, 2, 64], BF16, "pool")
    cosC = load("cosC", cosC_d, [64, 512], F32, "sp"); sinC = load("sinC", sinC_d, [64, 512], F32, "sp")
    pa = load("pa", pa_d, [128, 16, 33], F32, "sp"); pm = load("pm", pm_d, [128, 16, 33], F32, "sp")
    scm = load("scm", scm_d, [128, 16, 128], BF16, "pool"); sca = load("sca", sca_d, [128, 16, 128], BF16, "pool")
    Est = load("Est", E_d, [128, 65, 128], BF16, "pool"); cmpb = load("cmpb", cmpb_d, [128, 16, 2, 128], BF16, "pool")
    caus4 = load("caus4", caus_d, [128, 512], BF16, "pool"); low4 = load("low4", low_d, [128, 512], BF16, "pool")
    ident = load("ident", ident_d, [128, 128], F32, "sp")
    identb = load("identb", ident_d, [128, 128], BF16, "pool")
    kcT = sb("kcT", [66, 2, 512], BF16); vc = sb("vc", [128, 2, 4, 65], BF16)
    xs = sb("xs", [128, SS], BF16); hidT = sb("hidT", [128, 2, 512], BF16); posb = sb("posb", [128, 2], F32)
    t1 = sb("nt1", [64, 512], F32); t2 = sb("nt2", [64, 512], F32)
    ones_f = sb("ones_f", [65, 64], F32)
    P.op("dve", lambda e: e.memset(kcT[:], 0.0), writes=[kcT])
    P.op("dve", lambda e: e.memset(vc[:], 0.0), writes=[vc])
    P.op("dve", lambda e: e.memset(vc[:, :, :, 64:65], 1.0), writes=[vc])
    P.op("dve", lambda e: e.memset(ones_f[:], 1.0), writes=[ones_f])
    for g in range(2):
        for kv in "kv":
            P.dma("pool", lambda e, g=g, kv=kv: e.dma_start(out=xs[:, 0:S + 16], in_=xs_d[kv][:, g, :]), xs, writes=[xs])
            pb = bank[6]
            for hc in range(2):
                for m in range(16):
                    P.op("pe", lambda e, hc=hc, m=m, kv=kv: e.matmul(pb[:, hc:hc + 1], lhsT=w1[kv][:, m, hc * 128:(hc + 1) * 128],
                                                                     rhs=posf[kv][:, m:m + 1], start=(m == 0), stop=(m == 15)),
                         reads=[w1[kv], posf[kv]], writes=[pb])
            P.op("act", lambda e: e.copy(out=posb[:], in_=pb[:, 0:2]), reads=[pb], writes=[posb])
            for hc in range(2):
                ph = bank[4 + hc]
                for m in range(16):
                    P.op("pe", lambda e, hc=hc, m=m, kv=kv, ph=ph: e.matmul(
                        ph[:, 0:511], lhsT=w1[kv][:, m, hc * 128:(hc + 1) * 128],
                        rhs=xs[:, 2 * m:2 * m + 16 * 511].rearrange("p (n s) -> p n s", s=16)[:, :, 0],
                        start=(m == 0), stop=(m == 15)), reads=[w1[kv], xs], writes=[ph])
                P.op("act", lambda e, hc=hc, ph=ph: e.activation(out=hidT[:, hc, 0:511], in_=ph[:, 0:511], func=AF.Gelu_apprx_tanh,
                                                                 bias=posb[:, hc:hc + 1]), reads=[ph, posb], writes=[hidT])
            if kv == "k":
                pA, pB = bank[0], bank[1]
                for (p_, w_) in ((pA, w2k), (pB, w2ks)):
                    for hc in range(2):
                        P.op("pe", lambda e, p_=p_, w_=w_, hc=hc: e.matmul(p_[0:64, 0:511], lhsT=w_[:, hc, :], rhs=hidT[:, hc, 0:511],
                                                                           start=(hc == 0), stop=(hc == 1)), reads=[w_, hidT], writes=[p_])
                P.op("dve", lambda e: e.tensor_tensor(out=t1[:, 0:511], in0=pA[0:64, 0:511], in1=cosC[:, 0:511], op=ALU.mult),
                     reads=[pA, cosC], writes=[t1])
                P.op("dve", lambda e: e.tensor_tensor(out=t2[:, 0:511], in0=pB[0:64, 0:511], in1=sinC[:, 0:511], op=ALU.mult),
                     reads=[pB, sinC], writes=[t2])
                P.op("pool", lambda e, g=g: e.tensor_tensor(out=kcT[0:64, g, 0:511], in0=t1[:, 0:511], in1=t2[:, 0:511], op=ALU.add),
                     reads=[t1, t2], writes=[kcT])
            else:
                for t4 in range(4):
                    ncol = 128 if t4 < 3 else 127
                    p_ = bank[t4 % 2]
                    for hc in range(2):
                        P.op("pe", lambda e, p_=p_, hc=hc, t4=t4, ncol=ncol: e.matmul(
                            p_[0:ncol, 0:64], lhsT=hidT[:, hc, t4 * 128:t4 * 128 + ncol], rhs=w2v[:, hc, :],
                            start=(hc == 0), stop=(hc == 1)), reads=[hidT, w2v], writes=[p_])
                    P.op("act", lambda e, p_=p_, t4=t4, ncol=ncol, g=g: e.copy(out=vc[0:ncol, g, t4, 0:64], in_=p_[0:ncol, 0:64]),
                         reads=[p_], writes=[vc])
    qst = sb("qst", [66, 4, NT], BF16)
    ksT = xs; kwT = sb("kwT", [66, SS], BF16)
    vs = sb("vs", [128, 65, 65], BF16); vw = sb("vw", [128, 65, 65], BF16)
    s_sb = sb("s_sb", [128, 512], F32); p_sb = sb("p_sb", [128, 512], F32); acc = sb("acc", [128, 516], F32)
    mx = sb("mx", [128, 1], F32); nm = sb("nm", [128, 1], F32); sm1 = sb("sm1", [128, 1], F32); rinv = sb("rinv", [128, 1], F32)
    imp = sb("imp", [128, 128], F32); sc = sb("sc", [128, 128], F32); wk = sb("wk", [128, 128], F32)
    m8 = sb("m8", [128, 8], F32); m8b = sb("m8b", [128, 8], F32); thr = sb("thr", [128, 1], F32)
    negm = sb("negm", [128, 128], F32); negT4 = sb("negT4", [128, 512], BF16)
    cmpb4 = sb("cmpb4", [128, 2, 512], BF16)
    pT = [sb("pT%d" % i, [128, 512], BF16) for i in range(3)]
    o_sb = sb("no_sb", [65, 512], F32); frow = sb("frow", [65, 512], F32); gt = sb("gt", [65, 3, 512], F32)
    yacc = sb("yacc", [64, 512], F32); ytmp = sb("ytmp", [64, 512], F32)
    rot = [0, 0, 0]
    for g in range(2):
        P.dma("pool", lambda e, g=g: e.dma_start(out=qst[0:64, :, :], in_=qT_d[:, 4 * g:4 * g + 4, :]), qst, writes=[qst])
        P.op("dve", lambda e: e.memset(qst[64:66, :, :], 1.0), reads=[qst], writes=[qst])
        for (t_, d_) in ((ksT, ks_d), (kwT, kw_d)):
            P.dma("pool", lambda e, g=g, t_=t_, d_=d_: e.dma_start(out=t_[0:66, :], in_=d_[:, g, :]), t_, writes=[t_])
        for (t_, d_) in ((vs, vs_d), (vw, vw_d)):
            P.dma("pool", lambda e, g=g, t_=t_, d_=d_: e.dma_start(
                out=t_[:], in_=d_[:, g, :].rearrange("(s p) c -> p s c", p=128)), t_, writes=[t_])
        for i in range(NOWN):
            qs = 4 * (i + 1)
            qsl = slice(i * 128, (i + 1) * 128)
            lo = max(0, 32 * i - 2); hi = min(511, 32 * i + 31); nv = hi; wreg = hi - lo
            P.op("pool", lambda e: e.memset(acc[:], 0.0), writes=[acc])
            for r in range(4):
                pc = bank[4 + r % 2]
                P.op("pe", lambda e, pc=pc, r=r, qsl=qsl, nv=nv, g=g: e.matmul(pc[:, 0:nv], lhsT=qst[0:64, r, qsl], rhs=kcT[0:64, g, 0:nv],
                                                                        start=True, stop=True), reads=[qst, kcT], writes=[pc])
                P.op("act", lambda e, pc=pc, nv=nv: e.copy(out=s_sb[:, 0:nv], in_=pc[:, 0:nv]), reads=[pc], writes=[s_sb])
                P.op("dve", lambda e, lo=lo, hi=hi, i=i, wreg=wreg: e.tensor_tensor(out=s_sb[:, lo:hi], in0=s_sb[:, lo:hi], in1=pa[:, i, 0:wreg],
                                                                                    op=ALU.add), reads=[s_sb, pa], writes=[s_sb])
                P.op("dve", lambda e, nv=nv: e.reduce_max(out=mx[:], in_=s_sb[:, 0:nv], axis=AX.X), reads=[s_sb], writes=[mx])
                P.op("dve", lambda e: e.tensor_scalar(out=nm[:], in0=mx[:], scalar1=-1.0, scalar2=None, op0=ALU.mult), reads=[mx], writes=[nm])
                P.op("act", lambda e, nv=nv: e.activation(out=p_sb[:, 0:nv], in_=s_sb[:, 0:nv], func=AF.Exp, bias=nm[:, 0:1], scale=1.0),
                     reads=[s_sb, nm], writes=[p_sb])
                P.op("dve", lambda e, lo=lo, hi=hi, i=i, wreg=wreg: e.tensor_tensor(out=p_sb[:, lo:hi], in0=p_sb[:, lo:hi], in1=pm[:, i, 0:wreg],
                                                                                    op=ALU.mult), reads=[p_sb, pm], writes=[p_sb])
                P.op("dve", lambda e, nv=nv: e.reduce_sum(out=sm1[:], in_=p_sb[:, 0:nv], axis=AX.X), reads=[p_sb], writes=[sm1])
                P.op("dve", lambda e: e.tensor_scalar(out=sm1[:], in0=sm1[:], scalar1=1e-30, scalar2=None, op0=ALU.max), reads=[sm1], writes=[sm1])
                P.op("dve", lambda e: e.reciprocal(out=rinv[:], in_=sm1[:]), reads=[sm1], writes=[rinv])
                P.op("dve", lambda e, nv=nv: e.scalar_tensor_tensor(out=acc[:, 1:1 + nv], in0=p_sb[:, 0:nv], scalar=rinv[:, 0:1], in1=acc[:, 1:1 + nv],
                                                                    op0=ALU.mult, op1=ALU.add), reads=[p_sb, rinv, acc], writes=[acc])
            P.op("dve", lambda e: e.reduce_sum(out=imp[:], in_=acc[:, 0:512].rearrange("p (j f) -> p j f", f=4), axis=AX.X),
                 reads=[acc], writes=[imp])
            P.op("dve", lambda e: e.tensor_tensor(out=imp[:], in0=imp[:], in1=acc[:, 4:516].rearrange("p (j f) -> p j f", f=4)[:, :, 0],
                                                  op=ALU.add), reads=[imp, acc], writes=[imp])
            P.op("dve", lambda e, i=i: e.tensor_tensor(out=sc[:], in0=imp[:], in1=scm[:, i, :], op=ALU.mult), reads=[imp, scm], writes=[sc])
            P.op("dve", lambda e, i=i: e.tensor_tensor(out=sc[:], in0=sc[:], in1=sca[:, i, :], op=ALU.add), reads=[sc, sca], writes=[sc])
            P.op("dve", lambda e: e.max(out=m8[:], in_=sc[:]), reads=[sc], writes=[m8])
            P.op("dve", lambda e: e.match_replace(out=wk[:], in_to_replace=m8[:], in_values=sc[:], imm_value=-1e9),
                 reads=[m8, sc], writes=[wk])
            P.op("dve", lambda e: e.max(out=m8b[:], in_=wk[:]), reads=[wk], writes=[m8b])
            P.op("dve", lambda e: e.tensor_reduce(out=thr[:], in_=m8b[:], axis=AX.X, op=ALU.min), reads=[m8b], writes=[thr])
            P.op("dve", lambda e: e.tensor_scalar(out=negm[:], in0=sc[:], scalar1=thr[:, 0:1], scalar2=None, op0=ALU.is_ge),
                 reads=[sc, thr], writes=[negm])
            P.op("dve", lambda e: e.tensor_scalar(out=negm[:], in0=negm[:], scalar1=-NEGB, scalar2=NEGB, op0=ALU.mult, op1=ALU.add),
                 reads=[negm], writes=[negm])
            ptr = bank[6]
            P.op("pe", lambda e: e.transpose(out=ptr[:, 0:128], in_=negm[:], identity=ident[:]), reads=[negm, ident], writes=[ptr])
            for r in range(4):
                P.op("act", lambda e, r=r: e.copy(out=negT4[:, r * 128:(r + 1) * 128], in_=ptr[:, 0:128]), reads=[ptr], writes=[negT4])
            kt_lo = lo // 128; kt_hi = (hi - 1) // 128
            for e_ in range(kt_hi - kt_lo + 1):
                for r in range(4):
                    P.op("pool", lambda e, e_=e_, r=r, i=i: e.tensor_copy(out=cmpb4[:, e_, r * 128:(r + 1) * 128], in_=cmpb[:, i, e_, :]),
                         reads=[cmpb], writes=[cmpb4])
            P.dma("sp", lambda e, g=g, qsl=qsl: e.dma_start(out=gt[64:65, :, :].rearrange("p b (h q) -> p b h q", h=4),
                                                            in_=gate_d[0:1, :, 4 * g:4 * g + 4, qsl]), gt, writes=[gt])
            P.op("act", lambda e: e.activation(out=gt[64:65, :, :], in_=gt[64:65, :, :], func=AF.Sigmoid), reads=[gt], writes=[gt])

            def attn(tiles, br):
                po = bank[2 + rot[1] % 2]; rot[1] += 1
                nt_ = len(tiles)
                for ti, (k_ap, k_t, v_ap, v_t, biases) in enumerate(tiles):
                    ps_ = bank[rot[0] % 2]; rot[0] += 1
                    pt_ = pT[rot[2] % 3]; rot[2] += 1
                    P.op("pe", lambda e, ps_=ps_, k_ap=k_ap, qsl=qsl, biases=biases: e.matmul(ps_[:], lhsT=k_ap, rhs=qst[:, :, qsl],
                                                                     start=True, stop=(len(biases) == 0)), reads=[k_t, qst], writes=[ps_])
                    for bi, (l_ap, l_t, r_ap, r_t) in enumerate(biases):
                        P.op("pe", lambda e, ps_=ps_, l_ap=l_ap, r_ap=r_ap, bi=bi, nb=len(biases): e.matmul(
                            ps_[:], lhsT=l_ap, rhs=r_ap, start=False, stop=(bi == nb - 1)), reads=[l_t, r_t], writes=[ps_])
                    P.op("act", lambda e, ps_=ps_, pt_=pt_: e.activation(out=pt_[:], in_=ps_[:], func=AF.Exp), reads=[ps_], writes=[pt_])
                    P.op("pe", lambda e, po=po, v_ap=v_ap, pt_=pt_, ti=ti, nt_=nt_: e.matmul(po[0:65, :], lhsT=v_ap, rhs=pt_[:],
                                                                                          start=(ti == 0), stop=(ti == nt_ - 1)),
                         reads=[v_t, pt_], writes=[po])
                P.op("act", lambda e, po=po: e.copy(out=o_sb[:], in_=po[0:65, :]), reads=[po], writes=[o_sb])
                P.op("dve", lambda e: e.tensor_scalar(out=frow[64:65, :], in0=o_sb[64:65, :], scalar1=1e-30, scalar2=None, op0=ALU.max),
                     reads=[o_sb], writes=[frow])
                P.op("dve", lambda e: e.reciprocal(out=frow[64:65, :], in_=frow[64:65, :]), reads=[frow], writes=[frow])
                P.op("dve", lambda e, br=br: e.tensor_tensor(out=frow[64:65, :], in0=frow[64:65, :], in1=gt[64:65, br, :], op=ALU.mult),
                     reads=[frow, gt], writes=[frow])
                pbc = bank[7]
                P.op("pe", lambda e: e.matmul(pbc[0:64, :], lhsT=ones_f[64:65, :], rhs=frow[64:65, :], start=True, stop=True),
                     reads=[ones_f, frow], writes=[pbc])
                if br == 0:
                    P.op("dve", lambda e: e.tensor_tensor(out=yacc[:], in0=o_sb[0:64, :], in1=pbc[0:64, :], op=ALU.mult),
                         reads=[o_sb, pbc], writes=[yacc])
                else:
                    P.op("dve", lambda e: e.tensor_tensor(out=ytmp[:], in0=o_sb[0:64, :], in1=pbc[0:64, :], op=ALU.mult),
                         reads=[o_sb, pbc], writes=[ytmp])
                    P.op("pool", lambda e: e.tensor_tensor(out=yacc[:], in0=yacc[:], in1=ytmp[:], op=ALU.add), reads=[yacc, ytmp], writes=[yacc])

            tiles = []
            for kt in range((nv - 1) // 128 + 1):
                b_ = []
                if kt_lo <= kt <= kt_hi:
                    b_.append((identb[:], identb, cmpb4[:, kt - kt_lo, :], cmpb4))
                tiles.append((kcT[:, g, kt * 128:(kt + 1) * 128], kcT, vc[:, g, kt, :], vc, b_))
            attn(tiles, 0)
            tiles = []
            for s_ in range(qs + 1):
                b_ = [(Est[:, s_, :], Est, negT4[:], negT4)]
                if s_ == qs:
                    b_.append((identb[:], identb, caus4[:], caus4))
                tiles.append((ksT[0:66, s_ * 128:(s_ + 1) * 128], ksT, vs[:, s_, :], vs, b_))
            attn(tiles, 1)
            tiles = []
            for s_ in range(qs - 4, qs + 1):
                b_ = []
                if s_ == qs - 4:
                    b_.append((identb[:], identb, low4[:], low4))
                if s_ == qs:
                    b_.append((identb[:], identb, caus4[:], caus4))
                tiles.append((kwT[:, s_ * 128:(s_ + 1) * 128], kwT, vw[:, s_, :], vw, b_))
            attn(tiles, 2)
            finals.append(P.dma("sp", lambda e, g=g, qsl=qsl: e.dma_start(
                out=ynsa_d[:, 4 * g:4 * g + 4, qsl], in_=yacc[:].rearrange("p (h q) -> p h q", h=4)), yacc, reads=[yacc]))


def nsa_host_inputs(f, b, r, l, inputs):
    pad = 4 - r
    tok = ((np.arange(NOWN)[:, None] * 4 + r) * 128 + np.arange(128)[None, :]).reshape(-1)
    m = {}
    m["qT"] = np.ascontiguousarray(f[512:1024][:, tok].reshape(8, 64, NT).transpose(1, 0, 2))
    m["gate"] = np.ascontiguousarray(f[22 * 128:22 * 128 + 24][:, tok].reshape(1, 3, 8, NT))

    def kstream(rows):
        o = np.zeros((66, 2, SS), np.float32)
        n = min(S, SS - pad * 128)
        o[0:64, :, pad * 128:pad * 128 + n] = rows.reshape(2, 64, S).transpose(1, 0, 2)[:, :, :n]
        o[65, :, :pad * 128] = NEGB
        return o

    def vstream(rows):
        o = np.zeros((SS, 2, 65), np.float32)
        n = min(S, SS - pad * 128)
        o[pad * 128:pad * 128 + n, :, 0:64] = rows.reshape(2, 64, S).transpose(2, 0, 1)[:n]
        o[:, :, 64] = 1.0
        return o
    m["ksT"] = kstream(f[1024:1152]); m["kwT"] = kstream(f[1152:1280])
    m["vs"] = vstream(f[2560:2688]); m["vw"] = vstream(f[2688:2816])

    def xstack(rows):
        o = np.zeros((128, 2, S + 16), np.float32)
        a = rows.reshape(2, 64, S)
        o[0:64, :, :S] = a.transpose(1, 0, 2)
        o[64:128, :, :S - 1] = a.transpose(1, 0, 2)[:, :, 1:]
        return o
    m["xsK"] = xstack(f[2304:2432]); m["xsV"] = xstack(f[2432:2560])
    for kv, nm_ in (("k", "cmp_k"), ("v", "cmp_v")):
        m["w1" + kv] = np.ascontiguousarray(inputs[nm_ + "_w1"][l].reshape(16, 128, 256).transpose(1, 0, 2))
        m["pos" + kv] = np.ascontiguousarray(inputs[nm_ + "_pos"][l].reshape(16, 128).T)
    w2k = inputs["cmp_k_w2"][l]
    m["w2k"] = np.ascontiguousarray(w2k.reshape(2, 128, 64).transpose(1, 0, 2))
    m["w2ks"] = np.ascontiguousarray(np.concatenate([w2k[:, 32:], w2k[:, :32]], axis=1).reshape(2, 128, 64).transpose(1, 0, 2))
    m["w2v"] = np.ascontiguousarray(inputs["cmp_v_w2"][l].reshape(2, 128, 64).transpose(1, 0, 2))
    cpos = np.zeros(512, np.float32); cpos[:511] = np.arange(511) * 16 + 31
    cF, sS = rope_tables(cpos)
    m["cosC"] = np.ascontiguousarray(cF[:64]); m["sinC"] = np.ascontiguousarray(sS[:64])
    q = np.arange(128)
    pa = np.zeros((128, 16, 33), np.float32); pm = np.zeros((128, 16, 33), np.float32)
    scm = np.zeros((128, 16, 128), np.float32); sca = np.zeros((128, 16, 128), np.float32)
    cmpb = np.zeros((128, 16, 2, 128), np.float32)
    j = np.arange(128)
    for i in range(NOWN):
        qb = 4 * i + r
        t = 128 * qb + q
        lo = max(0, 32 * i - 2); hi = min(511, 32 * i + 31)
        n = lo + np.arange(33)
        valid = (16 * n[None, :] + 31 <= t[:, None]) & (n[None, :] < hi)
        pa[:, i, :] = np.where(valid, 0.0, NEGB); pm[:, i, :] = valid
        qblk = t // 64
        bv = j[None, :] <= qblk[:, None]
        forced = (j[None, :] == 0) | (bv & (j[None, :] > qblk[:, None] - 2))
        scm[:, i, :] = (bv & ~forced)
        sca[:, i, :] = np.where(forced, 100.0 + j[None, :], np.where(bv, 0.0, -1.0 - j[None, :]))
        kt_lo = lo // 128
        for e_ in range(2):
            nn = (kt_lo + e_) * 128 + np.arange(128)
            v2 = (nn[:, None] < 511) & (16 * nn[:, None] + 31 <= t[None, :])
            cmpb[:, i, e_, :] = np.where(v2, 0.0, NEGB)
    m.update(pa=pa, pm=pm, sc_mul=scm, sc_add=sca, cmpb=cmpb)
    E = np.zeros((128, 65, 128), np.float32)
    mm_ = np.arange(128)
    for s_ in range(65):
        rs_ = s_ - pad
        if 0 <= rs_ < 64:
            E[2 * rs_ + mm_ // 64, s_, mm_] = 1.0
    m["Estream"] = E
    caus = np.where(mm_[:, None] <= q[None, :], 0.0, NEGB).astype(np.float32)
    low = np.where(mm_[:, None] > q[None, :], 0.0, NEGB).astype(np.float32)
    m["causT4"] = np.ascontiguousarray(np.tile(caus, (1, 4))); m["lowT4"] = np.ascontiguousarray(np.tile(low, (1, 4)))
    m["ident"] = np.eye(128, dtype=np.float32)
    return m, tok


def kernel(**inputs):
    inputs = {k: np.asarray(v, dtype=np.float32) for k, v in inputs.items()}
    x = np.ascontiguousarray(inputs["x"])
    for l in range(2):
        pT = run_A(x, inputs["w_in"][l], inputs["norm_mix_pre"][l])
        ymix = run_B(pT, l, inputs)
        x = run_C(x, ymix, l, inputs)
    return x.astype(np.float32)
```
